# Optimizing a Trainium2 kernel written in Bass

```python
import jax, jax.numpy as jnp
from jax import lax
import numpy as np


D_MODEL = 2048
BATCH = 4
SEQ = 2048
DEPTH = 2

N_META = 16
CONV_WIDTH = D_MODEL
CONV_TAPS = 3
ATTN_HEADS = 32
ATTN_KV_HEADS = 8
ATTN_HEAD_DIM = D_MODEL // ATTN_HEADS
ATTN_GROUP = ATTN_HEADS // ATTN_KV_HEADS
WINDOW = 128
ATTN_BLOCK = 128
ROPE_THETA = 500000.0
ROPE_DIM = ATTN_HEAD_DIM // 4
MLSTM_HEADS = 8
MLSTM_V_DIM = D_MODEL // MLSTM_HEADS
MLSTM_QK_DIM = MLSTM_V_DIM // 2
MLSTM_CHUNK = 128
MLP_HIDDEN = 4 * D_MODEL
N_BRANCHES = 3
NORM_EPS = 1e-6
NEG_INF = -1e30
IN_SPLIT_SIZES = (CONV_WIDTH, CONV_WIDTH, CONV_WIDTH,
                  ATTN_HEADS * ATTN_HEAD_DIM, ATTN_KV_HEADS * ATTN_HEAD_DIM, ATTN_KV_HEADS * ATTN_HEAD_DIM,
                  MLSTM_HEADS * MLSTM_QK_DIM, MLSTM_HEADS * MLSTM_QK_DIM,
                  MLSTM_HEADS * MLSTM_V_DIM, MLSTM_HEADS * MLSTM_V_DIM,
                  MLSTM_HEADS, MLSTM_HEADS, MLSTM_HEADS, MLSTM_HEADS,
                  N_BRANCHES * D_MODEL)
IN_COLS = sum(IN_SPLIT_SIZES)

kernel_name = 'hybrid_gated_conv_swa_mlstm_encoder'


def rms_norm(x, g):
    xf = x.astype(jnp.float32)
    y = xf * lax.rsqrt(jnp.mean(xf * xf, axis=-1, keepdims=True) + NORM_EPS)
    return (y * g.astype(jnp.float32)).astype(x.dtype)


def rope_tables(length):
    pos = jnp.arange(length, dtype=jnp.float32)
    inv_freq = ROPE_THETA ** (-jnp.arange(0, ROPE_DIM, 2, dtype=jnp.float32) / ROPE_DIM)
    ang = pos[:, None] * inv_freq[None, :]
    return jnp.cos(ang), jnp.sin(ang)


def apply_partial_rope(x, cos, sin):
    half = ROPE_DIM // 2
    xr = x[..., :ROPE_DIM].astype(jnp.float32)
    x1, x2 = xr[..., :half], xr[..., half:]
    c = cos[None, :, None, :]
    s = sin[None, :, None, :]
    rot = jnp.concatenate([x1 * c - x2 * s, x2 * c + x1 * s], axis=-1).astype(x.dtype)
    return jnp.concatenate([rot, x[..., ROPE_DIM:]], axis=-1)


def short_conv(xc, bg, cg, w, b):
    u = cg * xc
    up = jnp.pad(u, ((0, 0), (1, 1), (0, 0)))
    y = up[:, :-2] * w[0] + up[:, 1:-1] * w[1] + up[:, 2:] * w[2] + b
    return bg * y


def sink_softmax(scores, visible, sink):
    scores = jnp.where(visible, scores, NEG_INF)
    mx = jnp.maximum(jnp.max(scores, axis=-1, keepdims=True), sink)
    p = jnp.exp(scores - mx)
    return p / (jnp.sum(p, axis=-1, keepdims=True) + jnp.exp(sink - mx))


def window_attention(q, k, v, sink):
    bsz, length = q.shape[0], q.shape[1]
    s_len = length - N_META
    nb = s_len // ATTN_BLOCK
    hd = ATTN_HEAD_DIM
    kvh = ATTN_KV_HEADS
    scale = hd ** -0.5
    q = q.reshape(bsz, length, kvh, ATTN_GROUP, hd)
    qm, qr = q[:, :N_META], q[:, N_META:]
    km, kr = k[:, :N_META], k[:, N_META:]
    vm, vr = v[:, :N_META], v[:, N_META:]
    sink_b = sink.astype(jnp.float32).reshape(kvh, ATTN_GROUP)

    def band(a):
        a = a.reshape(bsz, nb, ATTN_BLOCK, kvh, hd)
        a = jnp.pad(a, ((0, 0), (1, 1), (0, 0), (0, 0), (0, 0)))
        return jnp.concatenate([a[:, :-2], a[:, 1:-1], a[:, 2:]], axis=2)

    def with_meta(a_meta, a_band):
        a_meta = jnp.broadcast_to(a_meta[:, None], (bsz, nb, N_META, kvh, hd))
        return jnp.concatenate([a_meta, a_band], axis=2)

    kw = with_meta(km, band(kr))
    vw = with_meta(vm, band(vr))
    blk = jnp.arange(nb)
    qi = blk[:, None] * ATTN_BLOCK + jnp.arange(ATTN_BLOCK)[None, :]
    ki = (blk[:, None] - 1) * ATTN_BLOCK + jnp.arange(3 * ATTN_BLOCK)[None, :]
    vis = ((ki[:, None, :] >= 0) & (ki[:, None, :] < s_len)
           & (jnp.abs(qi[:, :, None] - ki[:, None, :]) <= WINDOW))
    vis = jnp.concatenate([jnp.ones((nb, ATTN_BLOCK, N_META), dtype=bool), vis], axis=-1)
    qb = qr.reshape(bsz, nb, ATTN_BLOCK, kvh, ATTN_GROUP, hd)
    sc = jnp.einsum('bnqkgd,bnskd->bnkgqs', qb, kw, preferred_element_type=jnp.float32) * scale
    pr = sink_softmax(sc, vis[None, :, None, None], sink_b[None, None, :, :, None, None])
    out_r = jnp.einsum('bnkgqs,bnskd->bnqkgd', pr.astype(v.dtype), vw)
    out_r = out_r.reshape(bsz, s_len, ATTN_HEADS, hd)

    km2 = jnp.concatenate([km, kr[:, :ATTN_BLOCK]], axis=1)
    vm2 = jnp.concatenate([vm, vr[:, :ATTN_BLOCK]], axis=1)
    mpos = jnp.arange(N_META)
    tpos = jnp.arange(ATTN_BLOCK)
    vis_m = jnp.concatenate([jnp.ones((N_META, N_META), dtype=bool),
                             (N_META + tpos[None, :] - mpos[:, None]) <= WINDOW], axis=-1)
    sc_m = jnp.einsum('bqkgd,bskd->bkgqs', qm, km2, preferred_element_type=jnp.float32) * scale
    pr_m = sink_softmax(sc_m, vis_m[None, None, None], sink_b[None, :, :, None, None])
    out_m = jnp.einsum('bkgqs,bskd->bqkgd', pr_m.astype(v.dtype), vm2)
    out_m = out_m.reshape(bsz, N_META, ATTN_HEADS, hd)
    return jnp.concatenate([out_m, out_r], axis=1)


def mlstm_chunkwise(q, k, v, log_i, log_f):
    bsz, t_len, nh, dqk = q.shape
    dv = v.shape[-1]
    nc = t_len // MLSTM_CHUNK

    def to_chunks(a):
        a = a.reshape((bsz, nc, MLSTM_CHUNK) + a.shape[2:])
        return jnp.moveaxis(a, 1, 0)

    xs = tuple(to_chunks(a) for a in (q, k, v, log_i, log_f))
    lower = jnp.tril(jnp.ones((MLSTM_CHUNK, MLSTM_CHUNK), dtype=bool))

    def step(carry, inp):
        c_mat, n_vec, m_st = carry
        qc, kc, vc, li, lf = inp
        bcum = jnp.cumsum(lf, axis=1)
        dmat = bcum[:, :, None, :] - bcum[:, None, :, :] + li[:, None, :, :]
        dmat = jnp.where(lower[None, :, :, None], dmat, NEG_INF)
        inter = bcum + m_st[:, None, :]
        m_t = jnp.maximum(inter, jnp.max(dmat, axis=2))
        wts = jnp.exp(dmat - m_t[:, :, None, :])
        sc = jnp.einsum('bthd,bshd->btsh', qc, kc) * wts
        a = jnp.exp(inter - m_t)
        num = (jnp.einsum('btsh,bshe->bthe', sc, vc)
               + a[..., None] * jnp.einsum('bthd,bhde->bthe', qc, c_mat))
        den = jnp.sum(sc, axis=2) + a * jnp.einsum('bthd,bhd->bth', qc, n_vec)
        den = jnp.maximum(jnp.abs(den), jnp.exp(-m_t))
        h = num / den[..., None]
        total = bcum[:, -1]
        g = total[:, None, :] - bcum + li
        m_new = jnp.maximum(total + m_st, jnp.max(g, axis=1))
        decay = jnp.exp(total + m_st - m_new)
        wg = jnp.exp(g - m_new[:, None, :])
        c_mat = decay[..., None, None] * c_mat + jnp.einsum('bsh,bshd,bshe->bhde', wg, kc, vc)
        n_vec = decay[..., None] * n_vec + jnp.einsum('bsh,bshd->bhd', wg, kc)
        return (c_mat, n_vec, m_new), h

    init = (jnp.zeros((bsz, nh, dqk, dv), jnp.float32),
            jnp.zeros((bsz, nh, dqk), jnp.float32),
            jnp.zeros((bsz, nh), jnp.float32))
    _, hs = lax.scan(step, init, xs)
    return jnp.moveaxis(hs, 0, 1).reshape(bsz, t_len, nh, dv)


def pad_time(a, n, front, value=0.0):
    cfg = [(0, 0)] * a.ndim
    cfg[1] = (n, 0) if front else (0, n)
    return jnp.pad(a, cfg, constant_values=value)


def mlstm_direction(q, k, v, log_i, log_f, reverse):
    pad = MLSTM_CHUNK - N_META
    length = q.shape[1]
    if reverse:
        q, k, v, log_i, log_f = [jnp.flip(a, axis=1) for a in (q, k, v, log_i, log_f)]
    front = not reverse
    h = mlstm_chunkwise(pad_time(q, pad, front), pad_time(k, pad, front), pad_time(v, pad, front),
                        pad_time(log_i, pad, front, NEG_INF), pad_time(log_f, pad, front))
    h = h[:, pad:] if front else h[:, :length]
    return jnp.flip(h, axis=1) if reverse else h


def mlstm_branch(q, k, v, o_pre, i_fw, f_fw, i_bw, f_bw, b_i, b_f, head_gain):
    bsz, length = q.shape[0], q.shape[1]
    f32 = jnp.float32
    qh = q.astype(f32).reshape(bsz, length, MLSTM_HEADS, MLSTM_QK_DIM)
    kh = k.astype(f32).reshape(bsz, length, MLSTM_HEADS, MLSTM_QK_DIM) * (MLSTM_QK_DIM ** -0.5)
    vh = v.astype(f32).reshape(bsz, length, MLSTM_HEADS, MLSTM_V_DIM)
    b_i = b_i.astype(f32)
    b_f = b_f.astype(f32)
    h_fw = mlstm_direction(qh, kh, vh, i_fw.astype(f32) + b_i[0],
                           jax.nn.log_sigmoid(f_fw.astype(f32) + b_f[0]), False)
    h_bw = mlstm_direction(qh, kh, vh, i_bw.astype(f32) + b_i[1],
                           jax.nn.log_sigmoid(f_bw.astype(f32) + b_f[1]), True)
    h = h_fw + h_bw
    h = h * lax.rsqrt(jnp.mean(h * h, axis=-1, keepdims=True) + NORM_EPS)
    h = h * head_gain.astype(f32).reshape(MLSTM_HEADS, MLSTM_V_DIM)
    h = h.reshape(bsz, length, MLSTM_HEADS * MLSTM_V_DIM) * jax.nn.sigmoid(o_pre.astype(f32))
    return h.astype(q.dtype)


def setup_inputs(seed: int = 0) -> dict:
    key = jax.random.key(seed)
    ks = jax.random.split(key, 20)
    nrm = jax.random.normal
    f32 = jnp.float32
    return {
        'x': nrm(ks[0], (BATCH, SEQ, D_MODEL), f32),
        'meta_tokens': nrm(ks[1], (N_META, D_MODEL), f32),
        'w_in': nrm(ks[2], (DEPTH, D_MODEL, IN_COLS), f32) * D_MODEL ** -0.5,
        'b_gate': nrm(ks[3], (DEPTH, N_BRANCHES * D_MODEL), f32) * 0.1,
        'conv_w': nrm(ks[4], (DEPTH, CONV_TAPS, CONV_WIDTH), f32) * 0.5,
        'conv_b': nrm(ks[5], (DEPTH, CONV_WIDTH), f32) * 0.01,
        'attn_sink': nrm(ks[6], (DEPTH, ATTN_HEADS), f32) * 0.5,
        'mlstm_b_i': nrm(ks[7], (DEPTH, 2, MLSTM_HEADS), f32) * 0.1,
        'mlstm_b_f': 3.0 + nrm(ks[8], (DEPTH, 2, MLSTM_HEADS), f32) * 0.5,
        'mlstm_head_gain': 1.0 + nrm(ks[9], (DEPTH, MLSTM_HEADS * MLSTM_V_DIM), f32) * 0.02,
        'w_conv_proj': nrm(ks[10], (DEPTH, CONV_WIDTH, D_MODEL), f32) * CONV_WIDTH ** -0.5,
        'w_attn_proj': nrm(ks[11], (DEPTH, ATTN_HEADS * ATTN_HEAD_DIM, D_MODEL), f32) * (ATTN_HEADS * ATTN_HEAD_DIM) ** -0.5,
        'w_mlstm_proj': nrm(ks[12], (DEPTH, MLSTM_HEADS * MLSTM_V_DIM, D_MODEL), f32) * (MLSTM_HEADS * MLSTM_V_DIM) ** -0.5,
        'w_out': nrm(ks[13], (DEPTH, D_MODEL, D_MODEL), f32) * D_MODEL ** -0.5,
        'norm_mix': 1.0 + nrm(ks[14], (DEPTH, D_MODEL), f32) * 0.02,
        'norm_mlp': 1.0 + nrm(ks[15], (DEPTH, D_MODEL), f32) * 0.02,
        'mlp_w1': nrm(ks[16], (DEPTH, D_MODEL, MLP_HIDDEN), f32) * D_MODEL ** -0.5,
        'mlp_w2': nrm(ks[17], (DEPTH, MLP_HIDDEN, D_MODEL), f32) * MLP_HIDDEN ** -0.5,
        'final_norm': 1.0 + nrm(ks[18], (D_MODEL,), f32) * 0.02,
    }


def reference(x, meta_tokens, w_in, b_gate, conv_w, conv_b, attn_sink, mlstm_b_i, mlstm_b_f,
              mlstm_head_gain, w_conv_proj, w_attn_proj, w_mlstm_proj, w_out, norm_mix, norm_mlp,
              mlp_w1, mlp_w2, final_norm):
    bsz = x.shape[0]
    meta = jnp.broadcast_to(meta_tokens[None].astype(x.dtype), (bsz, N_META, D_MODEL))
    h = jnp.concatenate([meta, x], axis=1)
    length = h.shape[1]
    cos, sin = rope_tables(length)
    split_at = np.cumsum(IN_SPLIT_SIZES)[:-1].tolist()
    for layer in range(DEPTH):
        xn = rms_norm(h, norm_mix[layer])
        w_parts = jnp.split(w_in[layer], split_at, axis=1)
        (c_x, c_b, c_c, a_q, a_k, a_v, m_q, m_k, m_v, m_o,
         m_if, m_ff, m_ib, m_fb, g_pre) = [xn @ w for w in w_parts]
        y_conv = short_conv(c_x, c_b, c_c, conv_w[layer], conv_b[layer])
        q = apply_partial_rope(a_q.reshape(bsz, length, ATTN_HEADS, ATTN_HEAD_DIM), cos, sin)
        k = apply_partial_rope(a_k.reshape(bsz, length, ATTN_KV_HEADS, ATTN_HEAD_DIM), cos, sin)
        v = a_v.reshape(bsz, length, ATTN_KV_HEADS, ATTN_HEAD_DIM)
        y_attn = window_attention(q, k, v, attn_sink[layer]).reshape(bsz, length, ATTN_HEADS * ATTN_HEAD_DIM)
        y_mlstm = mlstm_branch(m_q, m_k, m_v, m_o, m_if, m_ff, m_ib, m_fb,
                               mlstm_b_i[layer], mlstm_b_f[layer], mlstm_head_gain[layer])
        gates = jax.nn.sigmoid((g_pre + b_gate[layer]).astype(jnp.float32)).astype(x.dtype)
        gates = gates.reshape(bsz, length, N_BRANCHES, D_MODEL)
        merged = (gates[:, :, 0] * (y_conv @ w_conv_proj[layer])
                  + gates[:, :, 1] * (y_attn @ w_attn_proj[layer])
                  + gates[:, :, 2] * (y_mlstm @ w_mlstm_proj[layer]))
        h = h + merged @ w_out[layer]
        xn = rms_norm(h, norm_mlp[layer])
        h = h + jnp.square(jax.nn.relu(xn @ mlp_w1[layer])) @ mlp_w2[layer]
    return rms_norm(h, final_norm)[:, N_META:]
```

```python
import numpy as np
import os
from contextlib import ExitStack
import concourse.bass as bass
import concourse.mybir as mybir
from concourse.bass_utils import run_bass_kernel_spmd

F32 = mybir.dt.float32
BF16 = mybir.dt.bfloat16
AF = mybir.ActivationFunctionType
ALU = mybir.AluOpType
AX = mybir.AxisListType

D = 2048
NK = 16
DEPTH = 2
NCH = 16
LEAD = 16
T = LEAD + 128 * NCH
NT = NCH + 1
NCORES = 4
EPS = 1e-6
WSLOT = 8192
NSLOT = 3
IN_COLS = 21536
O_CX, O_CB, O_CC, O_AQ, O_AK, O_AV = 0, 2048, 4096, 6144, 8192, 8704
O_MQ, O_MK, O_MV, O_MO, O_MG, O_G = 9216, 10240, 11264, 13312, 15360, 15392


def tile_rng(tt):
    return (0, LEAD) if tt == 0 else (LEAD + 128 * (tt - 1), 128)


def groups(n, maxn=512, base=0):
    k = -(-n // maxn)
    q, r = divmod(n, k)
    out = []
    o = base
    for i in range(k):
        s = q + (1 if i < r else 0)
        out.append((o, s))
        o += s
    return out


class Buf:
    __slots__ = ("name", "w", "r", "excl")

    def __init__(self, name, excl=False):
        self.name = name
        self.w = {}
        self.r = {}
        self.excl = excl


class Prog:
    ENG = ("pe", "act", "dve", "pool", "sp")

    def __init__(self, dry, n_dma_sems=12):
        self.dry = dry
        self.ops = {e: [] for e in self.ENG}
        self.cnt = {e: 0 for e in self.ENG}
        self.known = {e: {} for e in self.ENG}
        self.dma_pool = {}
        self.dma_rr = {}
        for q in ("sp", "pool"):
            self.dma_pool[q] = ["d_%s_%d" % (q, i) for i in range(n_dma_sems)]
            self.dma_rr[q] = 0
            for k in self.dma_pool[q]:
                self.cnt[k] = 0

    def _waits(self, eng, reads, writes, part, extra=()):
        need = {}

        def add(k, v):
            if need.get(k, 0) < v:
                need[k] = v
        for b in reads:
            for k, v in b.w.items():
                add(k, v)
            if b.excl:
                for k, v in b.r.items():
                    if k != eng:
                        add(k, v)
        for b in writes:
            if not part:
                for k, v in b.w.items():
                    add(k, v)
            for k, v in b.r.items():
                add(k, v)
        for k, v in extra:
            add(k, v)
        out = []
        kn = self.known[eng]
        for k, v in need.items():
            if kn.get(k, 0) < v:
                kn[k] = v
                out.append((k, v))
        return out

    def _mark(self, key, v, reads, writes, part):
        for b in reads:
            if b.r.get(key, 0) < v:
                b.r[key] = v
        for b in writes:
            if part:
                b.w[key] = v
            else:
                b.w = {key: v}
                b.r = {}

    def op(self, eng, fn, reads=(), writes=(), part=False):
        if self.dry:
            return
        waits = self._waits(eng, reads, writes, part)
        self.cnt[eng] += 1
        self._mark(eng, self.cnt[eng], reads, writes, part)
        self.ops[eng].append((waits, fn, (eng, 1)))

    def dma(self, q, out, in_, reads=(), writes=(), part=False):
        if self.dry:
            return
        pool = self.dma_pool[q]
        k = pool[self.dma_rr[q] % len(pool)]
        self.dma_rr[q] += 1
        extra = ((k, self.cnt[k]),) if self.cnt[k] else ()
        waits = self._waits(q, reads, writes, part, extra)
        self.cnt[k] += 16
        self._mark(k, self.cnt[k], reads, writes, part)
        self.ops[q].append((waits, lambda e, o=out, i=in_: e.dma_start(out=o, in_=i), (k, 16)))

    def barrier(self):
        if self.dry:
            return
        for e in self.ENG:
            waits = []
            for k, v in self.cnt.items():
                if v > 0 and self.known[e].get(k, 0) < v:
                    self.known[e][k] = v
                    waits.append((k, v))
            self.ops[e].append((waits, None, None))

    def mm(self, items, reads, writes, part=False):
        def fn(e, items=items):
            ins = None
            for (o, l, r, s0, s1) in items:
                ins = e.matmul(o, l, r, start=s0, stop=s1)
            return ins
        self.op("pe", fn, reads, writes, part)

    def tp(self, out, in_, ident, reads, writes):
        self.op("pe", lambda e: e.transpose(out, in_, ident), reads, writes)

    def act(self, out, in_, func, reads, writes, bias=0.0, scale=1.0, part=False):
        self.op("act", lambda e: e.activation(out=out, in_=in_, func=func, bias=bias, scale=scale), reads, writes, part)

    def tt(self, eng, out, in0, in1, op, reads, writes, part=False):
        self.op(eng, lambda e: e.tensor_tensor(out=out, in0=in0, in1=in1, op=op), reads, writes, part)

    def ts(self, eng, out, in0, s1, s2, op0, op1, reads, writes, part=False):
        if s2 is None:
            self.op(eng, lambda e: e.tensor_scalar(out=out, in0=in0, scalar1=s1, scalar2=None, op0=op0), reads, writes, part)
        else:
            self.op(eng, lambda e: e.tensor_scalar(out=out, in0=in0, scalar1=s1, scalar2=s2, op0=op0, op1=op1), reads, writes, part)

    def stt(self, eng, out, in0, scalar, in1, op0, op1, reads, writes, part=False):
        self.op(eng, lambda e: e.scalar_tensor_tensor(out=out, in0=in0, scalar=scalar, in1=in1, op0=op0, op1=op1), reads, writes, part)

    def copy(self, eng, out, in_, reads, writes, part=False):
        if eng == "act":
            self.op(eng, lambda e: e.activation(out=out, in_=in_, func=AF.Copy), reads, writes, part)
        else:
            self.op(eng, lambda e: e.tensor_copy(out=out, in_=in_), reads, writes, part)

    def recip(self, out, in_, reads, writes, part=False):
        self.op("dve", lambda e: e.reciprocal(out=out, in_=in_), reads, writes, part)

    def memset(self, eng, ap, val, writes, part=False):
        self.op(eng, lambda e: e.memset(ap, val), (), writes, part)

    def emit(self, nc):
        keys = [k for k in self.cnt if self.cnt[k] > 0]
        with ExitStack() as st:
            sems = {k: st.enter_context(nc.semaphore("s_" + k)) for k in keys}
            block = st.enter_context(nc.Block())

            needed = {k: set() for k in self.ENG}
            for name in self.ENG:
                for waits, fn, inc in self.ops[name]:
                    for k, v in waits:
                        if k in needed:
                            needed[k].add(v)
            rank = {k: {v: i + 1 for i, v in enumerate(sorted(needed[k]))} for k in needed}

            def run(name, e):
                c = 0
                for waits, fn, inc in self.ops[name]:
                    for k, v in waits:
                        e.wait_ge(sems[k], rank[k][v] if k in rank else v)
                    if fn is not None:
                        ins = fn(e)
                        if inc[0] in rank:
                            c += 1
                            if c in needed[name]:
                                ins.then_inc(sems[name], 1)
                        else:
                            ins.then_inc(sems[inc[0]], inc[1])

            @block.tensor
            def _(e):
                run("pe", e)

            @block.scalar
            def _(e):
                run("act", e)

            @block.vector
            def _(e):
                run("dve", e)

            @block.gpsimd
            def _(e):
                run("pool", e)

            @block.sync
            def _(e):
                run("sp", e)


class WStream:
    def __init__(self, p, seq, uid, wst, slots):
        self.p, self.seq, self.uid, self.wst, self.slots = p, seq, uid, wst, slots
        self.rec = []
        self.loaded = 0

    def next(self, key, nkc, ncol, held=0):
        i = len(self.rec)
        self.rec.append((key, nkc * ncol))
        if self.p.dry:
            return None, None
        assert self.seq[i][0] == key
        while self.loaded < min(i - held + NSLOT, len(self.seq)):
            j = self.loaded
            k2, ne = self.seq[j]
            ap, b = self.slots[j % NSLOT]
            for c0 in range(0, ne, 4096):
                c1 = min(ne, c0 + 4096)
                self.p.dma("pool", ap[:, c0:c1], self.wst[self.uid[k2], :, c0:c1], writes=[b], part=c0 > 0)
            self.loaded += 1
        ap, b = self.slots[i % NSLOT]
        return ap[:, 0:nkc * ncol].rearrange("p (k c) -> p k c", k=nkc), b


class Arena:
    def __init__(self, ap, nbytes):
        self.ap, self.nbytes, self.off = ap, nbytes, 0

    def reset(self):
        self.off = 0

    def get(self, n, dt):
        sz = n * (4 if dt == F32 else 2)
        sz = (sz + 63) // 64 * 64
        o = self.off
        self.off += sz
        assert self.off <= self.nbytes, "arena overflow %d > %d" % (self.off, self.nbytes)
        v = self.ap[:, o // 2:(o + n * (4 if dt == F32 else 2)) // 2]
        return v.bitcast(F32) if dt == F32 else v


def cst_off():
    o = {}
    c = 0
    for l in range(DEPTH):
        for nm, n in (("gmix", 16), ("gmlp", 16), ("bgate", 48), ("convw", 48), ("convb", 16)):
            o[(nm, l)] = c
            c += n
    o["gfin"] = c
    c += 16
    return o, c


CST_OFF, NCST = cst_off()
REP_L = 2048 + 32 + 32
M_ID, M_PERM, M_ONES, M_LE4, M_GE4, M_META, NMAT = 0, 128, 256, 384, 896, 1408, 1472


class Builder:
    def __init__(self, p, ws, t):
        self.p, self.ws, self.t = p, ws, t
        self.ar = t["arena"]
        self.psi = 0

    def nextps(self):
        ap, b = self.t["ps"][self.psi % 8]
        self.psi += 1
        return ap, b

    def load_consts(self):
        p, t, ar = self.p, self.t, self.ar
        self.cst = t["carena"].get(NCST, F32)
        self.rep = t["carena"].get(DEPTH * REP_L, F32)
        self.tabs = t["carena"].get(2 * T, F32)
        self.mats = t["carena"].get(NMAT, BF16)
        self.matf = t["carena"].get(384, F32)
        self.esink = t["carena"].get(32 * DEPTH, F32)
        self.Bc = Buf("consts")
        p.dma("sp", self.cst, t["cst"], writes=[self.Bc], part=True)
        p.dma("sp", self.rep, t["rep"], writes=[self.Bc], part=True)
        p.dma("sp", self.tabs, t["tabs"], writes=[self.Bc], part=True)
        p.dma("sp", self.matf, t["matf"], writes=[self.Bc], part=True)
        p.dma("pool", self.mats, t["mats"], writes=[self.Bc], part=True)
        p.barrier()
        for l in range(DEPTH):
            o = l * REP_L + 2048 + 32
            p.act(self.esink[:, l * 32:(l + 1) * 32], self.rep[:, o:o + 32], AF.Exp, [self.Bc], [self.Bc], part=True)
        p.barrier()
        m = self.mats
        self.ident, self.permT, self.ones = m[:, M_ID:M_ID + 128], m[:, M_PERM:M_PERM + 128], m[:, M_ONES:M_ONES + 128]
        self.le4, self.ge4, self.mmeta = m[:, M_LE4:M_LE4 + 512], m[:, M_GE4:M_GE4 + 512], m[:, M_META:M_META + 64]
        self.trif = (self.matf[:, 0:128], self.matf[:, 128:256])
        self.onesf = self.matf[:, 256:384]

    def cs(self, nm, l, j, n=1):
        o = CST_OFF[(nm, l)] if l is not None else CST_OFF[nm]
        return self.cst[:, o + j:o + j + n]

    def rmsnorm(self, src, g0, n, gname, l, xn, Bxn, keep=None, Bkeep=None):
        p, ar = self.p, self.ar
        gs = groups(n)
        stg = [ar.get(n, F32) for _ in range(2)] if keep is None else None
        Bst = [Buf("st0"), Buf("st1")]
        sq = [ar.get(n, BF16) for _ in range(2)]
        Bsq = [Buf("sq0"), Buf("sq1")]
        rstd = ar.get(n, F32)
        Brs = Buf("rstd")
        pss = [self.nextps() for _ in gs]
        for kc in range(NK):
            if keep is None:
                h, Bh = stg[kc % 2], Bst[kc % 2]
            else:
                h, Bh = keep[:, kc, :], Bkeep
            p.dma("sp", h, src[kc * 128:(kc + 1) * 128, g0:g0 + n], writes=[Bh], part=keep is not None)
            p.act(sq[kc % 2], h, AF.Square, [Bh], [Bsq[kc % 2]])
            for gi, (o, s) in enumerate(gs):
                p.mm([(pss[gi][0][:, 0:s], self.ones, sq[kc % 2][:, o:o + s], kc == 0, kc == NK - 1)],
                     [Bsq[kc % 2], self.Bc], [pss[gi][1]], part=kc > 0)
        for gi, (o, s) in enumerate(gs):
            p.act(rstd[:, o:o + s], pss[gi][0][:, 0:s], AF.Sqrt, [pss[gi][1]], [Brs], bias=self.epsb, scale=1.0 / D, part=True)
        p.recip(rstd, rstd, [Brs], [Brs])
        for kc in range(NK):
            if keep is None:
                h, Bh = stg[kc % 2], Bst[kc % 2]
                p.dma("sp", h, src[kc * 128:(kc + 1) * 128, g0:g0 + n], writes=[Bh])
            else:
                h, Bh = keep[:, kc, :], Bkeep
            gcol = self.cs(gname, l, kc)
            p.stt("dve", xn[:, kc, :], h, gcol, rstd, ALU.mult, ALU.mult, [Bh, Brs, self.Bc], [Bxn], part=True)

    def phase_inproj(self, l, src):
        p, t, ar, ws = self.p, self.t, self.ar, self.ws
        ar.reset()
        xn = ar.get(NK * T, BF16).rearrange("p (k t) -> p k t", k=NK)
        Bxn = Buf("xn")
        mark = ar.off
        self.rmsnorm(src, 0, T, "gmix", l, xn, Bxn)
        p.barrier()
        if self.sub == "norm":
            for kc in range(NK):
                p.dma("sp", t["ycT"][kc * 128:(kc + 1) * 128, :], xn[:, kc, :], reads=[Bxn], writes=[t["B"]["ycT"]], part=True)
            p.barrier()
            return
        ar.off = mark
        gs = groups(T)
        u = ar.get(T + 2, F32)
        Bu = Buf("u")
        cxt = [ar.get(512, F32) for _ in range(2)]
        Bcxt = [Buf("cxt0"), Buf("cxt1")]
        acc = [ar.get(512, F32) for _ in range(2)]
        Bacc = [Buf("acc0"), Buf("acc1")]
        ys = [ar.get(T, BF16) for _ in range(2)]
        Bys = [Buf("ys0"), Buf("ys1")]
        Bd = t["B"]
        p.memset("dve", u, 0.0, [Bu])
        for j in range(NK):
            key = ("w_in", l, 0, NK, ((O_CX + j * 128, 128), (O_CB + j * 128, 128), (O_CC + j * 128, 128)))
            w, Bw = ws.next(key, NK, 384)
            if p.dry:
                continue
            for gi, (o, s) in enumerate(gs):
                px, Bpx = self.nextps()
                pc, Bpc = self.nextps()
                p.mm([(px[:, 0:s], w[:, kc, 0:128], xn[:, kc, o:o + s], kc == 0, kc == NK - 1) for kc in range(NK)], [Bw, Bxn], [Bpx])
                p.mm([(pc[:, 0:s], w[:, kc, 256:384], xn[:, kc, o:o + s], kc == 0, kc == NK - 1) for kc in range(NK)], [Bw, Bxn], [Bpc])
                c, Bc_ = cxt[gi % 2], Bcxt[gi % 2]
                p.copy("act", c[:, 0:s], px[:, 0:s], [Bpx], [Bc_])
                p.tt("dve", u[:, 1 + o:1 + o + s], pc[:, 0:s], c[:, 0:s], ALU.mult, [Bpc, Bc_], [Bu], part=True)
            y, By = ys[j % 2], Bys[j % 2]
            for gi, (o, s) in enumerate(gs):
                pb, Bpb = self.nextps()
                p.mm([(pb[:, 0:s], w[:, kc, 128:256], xn[:, kc, o:o + s], kc == 0, kc == NK - 1) for kc in range(NK)], [Bw, Bxn], [Bpb])
                a, Ba = acc[gi % 2], Bacc[gi % 2]
                p.ts("dve", a[:, 0:s], u[:, 1 + o:1 + o + s], self.cs("convw", l, 16 + j), self.cs("convb", l, j), ALU.mult, ALU.add, [Bu, self.Bc], [Ba])
                p.stt("dve", a[:, 0:s], u[:, o:o + s], self.cs("convw", l, j), a[:, 0:s], ALU.mult, ALU.add, [Bu, Ba], [Ba])
                p.stt("dve", a[:, 0:s], u[:, 2 + o:2 + o + s], self.cs("convw", l, 32 + j), a[:, 0:s], ALU.mult, ALU.add, [Bu, Ba], [Ba])
                p.tt("dve", y[:, o:o + s], pb[:, 0:s], a[:, 0:s], ALU.mult, [Bpb, Ba], [By], part=True)
            p.dma("sp", t["ycT"][j * 128:(j + 1) * 128, :], y, reads=[By], writes=[Bd["ycT"]], part=True)
        if self.sub == "a":
            p.barrier()
            return
        ctab, stab = self.tabs[:, 0:T], self.tabs[:, T:2 * T]
        qb = [ar.get(512, BF16) for _ in range(2)]
        Bqb = [Buf("qb0"), Buf("qb1")]
        t1 = [ar.get(512, F32) for _ in range(2)]
        Bt1 = [Buf("t10"), Buf("t11")]
        t2 = [ar.get(512, F32) for _ in range(2)]
        Bt2 = [Buf("t20"), Buf("t21")]
        for pn in range(5):
            segs = []
            for cc in range(4):
                ch = pn * 4 + cc
                if ch < 16:
                    c, g = ch // 4, ch % 4
                    segs += [(O_AQ + (4 * (2 * c) + g) * 64, 64), (O_AQ + (4 * (2 * c + 1) + g) * 64, 64)]
                else:
                    segs += [(O_AK + (ch - 16) * 128, 128)]
            w, Bw = ws.next(("w_in", l, 0, NK, tuple(segs)), NK, 512)
            if p.dry:
                continue
            for cc in range(4):
                ch = pn * 4 + cc
                y, By = ys[ch % 2], Bys[ch % 2]
                for gi, (o, s) in enumerate(gs):
                    pq, Bpq = self.nextps()
                    p.mm([(pq[:, 0:s], w[:, kc, cc * 128:(cc + 1) * 128], xn[:, kc, o:o + s], kc == 0, kc == NK - 1) for kc in range(NK)], [Bw, Bxn], [Bpq])
                    q_, Bq_ = qb[gi % 2], Bqb[gi % 2]
                    p.copy("act", q_[:, 0:s], pq[:, 0:s], [Bpq], [Bq_])
                    p2, Bp2 = self.nextps()
                    p.mm([(p2[:, 0:s], self.permT, q_[:, 0:s], True, True)], [Bq_, self.Bc], [Bp2])
                    a1, Ba1 = t1[gi % 2], Bt1[gi % 2]
                    a2, Ba2 = t2[gi % 2], Bt2[gi % 2]
                    if os.environ.get("DBG_NOROPE"):
                        p.copy("dve", y[:, o:o + s], pq[:, 0:s], [Bpq, Bp2], [By], part=True)
                        continue
                    p.tt("dve", a1[:, 0:s], pq[:, 0:s], ctab[:, o:o + s], ALU.mult, [Bpq, self.Bc], [Ba1])
                    if os.environ.get("DBG_ROPE1"):
                        p.copy("dve", y[:, o:o + s], a1[:, 0:s], [Ba1, Bp2], [By], part=True)
                        continue
                    p.tt("dve", a2[:, 0:s], p2[:, 0:s], stab[:, o:o + s], ALU.mult, [Bp2, self.Bc], [Ba2])
                    p.tt("dve", y[:, o:o + s], a1[:, 0:s], a2[:, 0:s], ALU.add, [Ba1, Ba2], [By], part=True)
                p.dma("sp", t["qkT"][ch * 128:(ch + 1) * 128, :], y, reads=[By], writes=[Bd["qkT"]], part=True)
        if self.sub == "b":
            p.barrier()
            return
        stk = [ar.get(1024, BF16) for _ in range(3)]
        Bstk = [Buf("stk%d" % i) for i in range(3)]
        stg32 = [ar.get(32, F32) for _ in range(2)]
        Bstg32 = [Buf("sg0"), Buf("sg1")]
        plist = [("va", O_AV, 512, 0)]
        plist += [("mq", O_MQ + i * 512, 512, i * 512) for i in range(2)]
        plist += [("mk", O_MK + i * 512, 512, i * 512) for i in range(2)]
        plist += [("mv", O_MV + i * 512, 512, i * 512) for i in range(4)]
        plist += [("mo", O_MO + i * 512, 512, i * 512) for i in range(4)]
        plist += [("mg", O_MG, 32, 0)]
        si = 0
        for (kind, c0, nc_, d0) in plist:
            w, Bw = ws.next(("w_in", l, 0, NK, ((c0, nc_),)), NK, nc_)
            if p.dry:
                continue
            for tt in range(NT):
                tok0, ntok = tile_rng(tt)
                ps, Bps = self.nextps()
                p.mm([(ps[:ntok, 0:nc_], xn[:, kc, tok0:tok0 + ntok], w[:, kc, :], kc == 0, kc == NK - 1) for kc in range(NK)], [Bw, Bxn], [Bps])
                if kind == "mg":
                    s_, Bs_ = stg32[tt % 2], Bstg32[tt % 2]
                    p.copy("dve", s_[:ntok, :], ps[:ntok, 0:32], [Bps], [Bs_])
                    p.dma("sp", t["mg"][tok0:tok0 + ntok, :], s_[:ntok, :], reads=[Bs_], writes=[Bd["mg"]], part=True)
                    continue
                s_, Bs_ = stk[si % 3], Bstk[si % 3]
                si += 1
                if kind == "va":
                    s4 = s_.rearrange("p (h r d) -> p h r d", h=8, r=2)
                    p3 = ps[:, 0:512].rearrange("p (h d) -> p h d", h=8)
                    p.copy("act", s4[:ntok, :, 0, :], p3[:ntok], [Bps], [Bs_], part=True)
                    p.copy("dve", s4[:ntok, :, 1, :], p3[:ntok], [Bps], [Bs_], part=True)
                    p.dma("sp", t["vA"][tok0:tok0 + ntok, :], s_[:ntok, :], reads=[Bs_], writes=[Bd["vA"]], part=True)
                    continue
                eng = "act" if (tt % 2 == 0 or kind in ("mk", "mo")) else "dve"
                if kind == "mk":
                    p.act(s_[:ntok, 0:512], ps[:ntok, 0:512], AF.Copy, [Bps], [Bs_], scale=128.0 ** -0.5)
                elif kind == "mo":
                    p.act(s_[:ntok, 0:512], ps[:ntok, 0:512], AF.Sigmoid, [Bps], [Bs_])
                else:
                    p.copy(eng, s_[:ntok, 0:512], ps[:ntok, 0:512], [Bps], [Bs_])
                p.dma("sp", t[kind][tok0:tok0 + ntok, d0:d0 + 512], s_[:ntok, 0:512], reads=[Bs_], writes=[Bd[kind]], part=True)
        if self.sub == "c":
            p.barrier()
            return
        for pn in range(12):
            w, Bw = ws.next(("w_in", l, 0, NK, ((O_G + pn * 512, 512),)), NK, 512)
            if p.dry:
                continue
            for cc in range(4):
                ch = pn * 4 + cc
                y, By = ys[ch % 2], Bys[ch % 2]
                for gi, (o, s) in enumerate(gs):
                    pg, Bpg = self.nextps()
                    p.mm([(pg[:, 0:s], w[:, kc, cc * 128:(cc + 1) * 128], xn[:, kc, o:o + s], kc == 0, kc == NK - 1) for kc in range(NK)], [Bw, Bxn], [Bpg])
                    p.act(y[:, o:o + s], pg[:, 0:s], AF.Sigmoid, [Bpg, self.Bc], [By], bias=self.cs("bgate", l, ch), part=True)
                p.dma("sp", t["gT"][ch * 128:(ch + 1) * 128, :], y, reads=[By], writes=[Bd["gT"]], part=True)
        p.barrier()

    def phase_attn(self, l):
        p, t, ar = self.p, self.t, self.ar
        Bd = t["B"]
        ar.reset()
        Vd = ar.get(NT * 1024, BF16).rearrange("p (t h c) -> p t h c", t=NT, h=8)
        BV = Buf("Vd")
        for tt in range(NT):
            tok0, ntok = tile_rng(tt)
            p.dma("sp", Vd[:ntok, tt], t["vA"][tok0:tok0 + ntok, :].rearrange("p (h c) -> p h c", h=8), reads=[Bd["vA"]], writes=[BV], part=True)
        KT = [ar.get(T, BF16) for _ in range(2)]
        BKT = [Buf("KT0"), Buf("KT1")]
        Q4 = [ar.get(4 * T, BF16).rearrange("p (g t) -> p g t", g=4) for _ in range(2)]
        BQ4 = [Buf("Q40"), Buf("Q41")]
        PT = [ar.get(512, BF16) for _ in range(8)]
        BPT = [Buf("PT%d" % i) for i in range(8)]
        YS = [ar.get(2 * T, BF16).rearrange("p (c t) -> p c t", c=2) for _ in range(2)]
        BYS = [Buf("YS0"), Buf("YS1")]
        RD = [ar.get(512, F32) for _ in range(2)]
        BRD = [Buf("RD0"), Buf("RD1")]
        pti = 0
        it = 0
        for c in range(4):
            kt_, Bkt = KT[c % 2], BKT[c % 2]
            q4, Bq4 = Q4[c % 2], BQ4[c % 2]
            p.dma("sp", kt_, t["qkT"][2048 + c * 128:2048 + (c + 1) * 128, :], reads=[Bd["qkT"]], writes=[Bkt])
            p.dma("sp", q4, t["qkT"][c * 512:(c + 1) * 512, :].rearrange("(g p) t -> p g t", p=128), reads=[Bd["qkT"]], writes=[Bq4])
            for e_ in range(2):
                kvh = 2 * c + e_
                r0 = 64 * e_
                ysb, Bys = YS[kvh % 2], BYS[kvh % 2]
                for qt in range(NT):
                    tok0, nq = tile_rng(qt)
                    n4 = 4 * nq
                    if qt == 0:
                        blks = [(0, None), (1, self.mmeta)]
                    else:
                        blks = [(0, None)]
                        if qt - 1 >= 1:
                            blks.append((qt - 1, self.ge4))
                        blks.append((qt, None))
                        if qt + 1 <= NCH:
                            blks.append((qt + 1, self.le4))
                    rhs = q4[r0:r0 + 64, :, tok0:tok0 + nq]
                    pts = []
                    for (kt, mask) in blks:
                        k0, nk = tile_rng(kt)
                        ps, Bps = self.nextps()
                        p.mm([(ps[:nk, 0:n4], kt_[r0:r0 + 64, k0:k0 + nk], rhs, True, True)], [Bkt, Bq4], [Bps])
                        pt, Bpt = PT[pti % 8], BPT[pti % 8]
                        pti += 1
                        p.act(pt[:nk, 0:n4], ps[:nk, 0:n4], AF.Exp, [Bps], [Bpt], scale=0.125)
                        if mask is not None:
                            p.tt("dve", pt[:nk, 0:n4], pt[:nk, 0:n4], mask[:nk, 0:n4], ALU.mult, [Bpt, self.Bc], [Bpt])
                        pts.append((pt, Bpt, kt, nk))
                    pn_, Bpn = self.nextps()
                    pd_, Bpd = self.nextps()
                    nb = len(pts)
                    p.mm([(pn_[:, 0:n4], Vd[:nk, kt, kvh, :], pt[:nk, 0:n4], i == 0, i == nb - 1) for i, (pt, Bpt, kt, nk) in enumerate(pts)],
                         [BV] + [x[1] for x in pts], [Bpn])
                    p.mm([(pd_[:, 0:n4], self.ones[:nk, :], pt[:nk, 0:n4], i == 0, i == nb - 1) for i, (pt, Bpt, kt, nk) in enumerate(pts)],
                         [self.Bc] + [x[1] for x in pts], [Bpd])
                    rd, Brd = RD[it % 2], BRD[it % 2]
                    it += 1
                    for g in range(4):
                        h = 4 * kvh + g
                        p.ts("dve", rd[:, g * nq:(g + 1) * nq], pd_[:, g * nq:(g + 1) * nq], self.esink[:, l * 32 + h:l * 32 + h + 1], None, ALU.add, None,
                             [Bpd, self.Bc], [Brd], part=True)
                    p.recip(rd[:, 0:n4], rd[:, 0:n4], [Brd], [Brd])
                    pn3 = pn_[:, 0:n4].rearrange("p (g q) -> p g q", g=4)
                    rd3 = rd[:, 0:n4].rearrange("p (g q) -> p g q", g=4)
                    for par in range(2):
                        rr = slice(par * 64, (par + 1) * 64)
                        p.tt("dve", ysb[rr, :, tok0:tok0 + nq], pn3[rr, par::2, :], rd3[rr, par::2, :], ALU.mult, [Bpn, Brd], [Bys], part=True)
                p.dma("sp", t["yaT"][kvh * 256:(kvh + 1) * 256, :].rearrange("(c p) t -> p c t", p=128), ysb, reads=[Bys], writes=[Bd["yaT"]], part=True)
        p.barrier()

    def phase_mlstm(self, l):
        p, t, ar = self.p, self.t, self.ar
        Bd = t["B"]
        ar.reset()
        rep0 = l * REP_L
        gain = self.rep[:, rep0:rep0 + 2048]
        gbias = self.rep[:, rep0 + 2048:rep0 + 2080]
        PRE = ar.get(NT * 32, F32).rearrange("p (t c) -> p t c", t=NT)
        LF = ar.get(NT * 16, F32).rearrange("p (t c) -> p t c", t=NT)
        A_ = ar.get(NT * 16, F32).rearrange("p (t c) -> p t c", t=NT)
        B_ = ar.get(NT * 16, F32).rearrange("p (t c) -> p t c", t=NT)
        B2 = ar.get(NT * 16, F32).rearrange("p (t c) -> p t c", t=NT)
        ET = ar.get(NT * 16, F32).rearrange("p (t c) -> p t c", t=NT)
        X = [ar.get(16, F32) for _ in range(2)]
        X2 = [ar.get(16, F32) for _ in range(2)]
        Bg = Buf("gates")
        BX = [Buf("X0"), Buf("X1")]
        for tt in range(NT):
            tok0, ntok = tile_rng(tt)
            p.dma("sp", PRE[:ntok, tt, :], t["mg"][tok0:tok0 + ntok, :], reads=[Bd["mg"]], writes=[Bg], part=True)
        p.barrier()
        for tt in range(NT):
            tok0, ntok = tile_rng(tt)
            pre = PRE[:ntok, tt, :]
            p.tt("dve", pre, pre, gbias[:ntok, :], ALU.add, [Bg, self.Bc], [Bg], part=True)
            pre4 = pre.rearrange("p (d k h) -> p d k h", d=2, k=2)
            lf3 = LF[:ntok, tt, :].rearrange("p (d h) -> p d h", d=2)
            p.act(lf3, pre4[:, :, 1, :], AF.Exp, [Bg], [Bg], scale=-1.0, part=True)
            p.act(lf3, lf3, AF.Ln, [Bg], [Bg], bias=self.oneb[:ntok, :], part=True)
            ps, Bps = self.nextps()
            items = []
            for d in range(2):
                items.append((ps[:ntok, d * 8:(d + 1) * 8], self.trif[d][:ntok, :ntok], LF[:ntok, tt, d * 8:(d + 1) * 8], True, True))
            items.append((ps[:, 16:32], self.onesf[:ntok, :], LF[:ntok, tt, :], True, True))
            p.mm(items, [Bg, self.Bc], [Bps])
            x, x2, Bx = X[tt % 2], X2[tt % 2], BX[tt % 2]
            p.act(A_[:ntok, tt, :], ps[:ntok, 0:16], AF.Exp, [Bps], [Bg], scale=-1.0, part=True)
            p.tt("dve", x[:ntok, :].rearrange("p (d h) -> p d h", d=2), ps[:ntok, 0:16].rearrange("p (d h) -> p d h", d=2), pre4[:, :, 0, :], ALU.add, [Bps, Bg], [Bx])
            p.act(B_[:ntok, tt, :], x[:ntok, :], AF.Exp, [Bx], [Bg], part=True)
            p.tt("dve", x2[:ntok, :], x[:ntok, :], ps[:ntok, 16:32], ALU.subtract, [Bps, Bx], [Bx], part=True)
            p.act(B2[:ntok, tt, :], x2[:ntok, :], AF.Exp, [Bx], [Bg], part=True)
            p.act(ET[:, tt, :], ps[:, 16:32], AF.Exp, [Bps], [Bg], scale=-1.0, part=True)
        p.barrier()
        if self.sub == "gates":
            for tt in range(NT):
                tok0, ntok = tile_rng(tt)
                p.dma("sp", t["mq"][tok0:tok0 + ntok, 0:16].bitcast(F32) if False else t["dbgf"][tok0:tok0 + ntok, 0:16], A_[:ntok, tt, :], reads=[Bg], writes=[Bd["mq"]], part=True)
                p.dma("sp", t["dbgf"][tok0:tok0 + ntok, 16:32], B_[:ntok, tt, :], reads=[Bg], writes=[Bd["mq"]], part=True)
                p.dma("sp", t["dbgf"][tok0:tok0 + ntok, 32:48], B2[:ntok, tt, :], reads=[Bg], writes=[Bd["mq"]], part=True)
                p.dma("sp", t["dbgf"][tok0:tok0 + ntok, 48:64], ET[:ntok, tt, :], reads=[Bg], writes=[Bd["mq"]], part=True)
            p.barrier()
            return
        qh = [ar.get(NT * 128, BF16).rearrange("p (t c) -> p t c", t=NT) for _ in range(2)]
        kh = [ar.get(NT * 128, BF16).rearrange("p (t c) -> p t c", t=NT) for _ in range(2)]
        vh = [ar.get(NT * 258, BF16).rearrange("p (t c) -> p t c", t=NT) for _ in range(2)]
        oh = [ar.get(NT * 256, BF16).rearrange("p (t c) -> p t c", t=NT) for _ in range(2)]
        Bin = [Buf("min0"), Buf("min1")]
        hsum = ar.get(NT * 256, F32).rearrange("p (t c) -> p t c", t=NT)
        Bhs = Buf("hsum")
        QS = [ar.get(128, BF16) for _ in range(2)]
        KS = [ar.get(128, BF16) for _ in range(2)]
        K2 = [ar.get(128, BF16) for _ in range(2)]
        QST = [ar.get(128, BF16) for _ in range(2)]
        KST = [ar.get(128, BF16) for _ in range(2)]
        PTm = [ar.get(128, BF16) for _ in range(2)]
        C = [ar.get(260, F32) for _ in range(2)]
        Cb = [ar.get(260, BF16) for _ in range(2)]
        dn = [ar.get(2, F32) for _ in range(2)]
        BQS = [Buf("QS0"), Buf("QS1")]
        BKS = [Buf("KS0"), Buf("KS1")]
        BK2 = [Buf("K20"), Buf("K21")]
        BQST = [Buf("QST0"), Buf("QST1")]
        BKST = [Buf("KST0"), Buf("KST1")]
        BPTm = [Buf("PTm0"), Buf("PTm1")]
        BC = [Buf("C0"), Buf("C1")]
        BCb = [Buf("Cb0"), Buf("Cb1")]
        Bdn = [Buf("dn0"), Buf("dn1")]
        ysm = [ar.get(2 * T, BF16).rearrange("p (c t) -> p c t", c=2) for _ in range(2)]
        Bysm = [Buf("ysm0"), Buf("ysm1")]
        junk = ar.get(256, F32)
        Bjunk = Buf("junk")
        ssq = ar.get(2, F32)
        Bssq = Buf("ssq")
        tmpy = ar.get(256, F32)
        Btmpy = Buf("tmpy")
        ym = ar.get(256, BF16)
        Bym = Buf("ym")
        masks = (self.le4[:, 0:128], self.ge4[:, 0:128])
        for x_ in vh:
            p.memset("dve", x_[:, :, 256:258], 1.0, [Bin[0], Bin[1]], part=True)
        p.barrier()

        def load_head(hh):
            b = hh % 2
            for (dst, src, w_) in ((qh[b], t["mq"], 128), (kh[b], t["mk"], 128), (vh[b], t["mv"], 256), (oh[b], t["mo"], 256)):
                nm = "mq" if src is t["mq"] else "mk" if src is t["mk"] else "mv" if src is t["mv"] else "mo"
                for tt in range(NT):
                    tok0, ntok = tile_rng(tt)
                    p.dma("sp", dst[0:ntok, tt, 0:w_], src[tok0:tok0 + ntok, hh * w_:(hh + 1) * w_], reads=[Bd[nm]], writes=[Bin[b]], part=True)

        load_head(0)
        NH_ = int(os.environ.get("MLSTM_NH", "8"))
        NI_ = int(os.environ.get("MLSTM_NI", str(NT)))
        POST_ = int(os.environ.get("MLSTM_POST", "1"))
        for hh in range(NH_):
            b = hh % 2
            if hh + 1 < NH_:
                load_head(hh + 1)
            visited = set()
            for i in range(NI_):
                for d in range(2):
                    tt = i if d == 0 else NT - 1 - i
                    tok0, ntok = tile_rng(tt)
                    col = d * 8 + hh
                    first = i == 0
                    p.act(QS[d][:ntok, :], qh[b][:ntok, tt, :], AF.Copy, [Bin[b], Bg], [BQS[d]], scale=A_[:ntok, tt, col:col + 1])
                    p.act(KS[d][:ntok, :], kh[b][:ntok, tt, :], AF.Copy, [Bin[b], Bg], [BKS[d]], scale=B_[:ntok, tt, col:col + 1])
                    p.act(K2[d][:ntok, :], kh[b][:ntok, tt, :], AF.Copy, [Bin[b], Bg], [BK2[d]], scale=B2[:ntok, tt, col:col + 1])
                    ps1, Bps1 = self.nextps()
                    p.mm([(ps1[:, 0:ntok], QS[d][:ntok, :], self.ident[:ntok, :ntok], True, True)], [BQS[d], self.Bc], [Bps1])
                    p.copy("dve", QST[d][:, 0:ntok], ps1[:, 0:ntok], [Bps1], [BQST[d]])
                    ps2, Bps2 = self.nextps()
                    p.mm([(ps2[:, 0:ntok], KS[d][:ntok, :], self.ident[:ntok, :ntok], True, True)], [BKS[d], self.Bc], [Bps2])
                    p.copy("dve", KST[d][:, 0:ntok], ps2[:, 0:ntok], [Bps2], [BKST[d]])
                    pss, Bpss = self.nextps()
                    p.mm([(pss[:ntok, 0:ntok], KST[d][:, 0:ntok], QST[d][:, 0:ntok], True, True)], [BKST[d], BQST[d]], [Bpss])
                    p.tt("dve", PTm[d][:ntok, 0:ntok], pss[:ntok, 0:ntok], masks[d][:ntok, 0:ntok], ALU.mult, [Bpss, self.Bc], [BPTm[d]])
                    pso, Bpso = self.nextps()
                    items = [(pso[:ntok, 0:257], PTm[d][:ntok, 0:ntok], vh[b][:ntok, tt, 0:257], True, first)]
                    rds = [BPTm[d], Bin[b]]
                    if not first:
                        items.append((pso[:ntok, 0:257], QST[d][:, 0:ntok], Cb[d][:, 0:257], False, True))
                        rds += [BQST[d], BCb[d]]
                    p.mm(items, rds, [Bpso])
                    p.act(dn[d][:ntok, 0:1], pso[:ntok, 256:257], AF.Abs, [Bpso], [Bdn[d]])
                    p.ts("dve", dn[d][:ntok, 0:1], dn[d][:ntok, 0:1], 1.0, None, ALU.max, None, [Bdn[d]], [Bdn[d]])
                    p.recip(dn[d][:ntok, 0:1], dn[d][:ntok, 0:1], [Bdn[d]], [Bdn[d]])
                    if tt not in visited:
                        visited.add(tt)
                        p.act(hsum[:ntok, tt, :], pso[:ntok, 0:256], AF.Copy, [Bpso, Bdn[d]], [Bhs], scale=dn[d][:ntok, 0:1], part=True)
                    else:
                        p.stt("dve", hsum[:ntok, tt, :], pso[:ntok, 0:256], dn[d][:ntok, 0:1], hsum[:ntok, tt, :], ALU.mult, ALU.add, [Bpso, Bdn[d], Bhs], [Bhs], part=True)
                    psc, Bpsc = self.nextps()
                    p.mm([(psc[:, 0:257], K2[d][:ntok, :], vh[b][:ntok, tt, 0:257], True, True)], [BK2[d], Bin[b]], [Bpsc])
                    if first:
                        p.copy("dve", C[d][:, 0:257], psc[:, 0:257], [Bpsc], [BC[d]])
                    else:
                        p.stt("dve", C[d][:, 0:257], C[d][:, 0:257], ET[:, tt, col:col + 1], psc[:, 0:257], ALU.mult, ALU.add, [Bpsc, BC[d], Bg], [BC[d]])
                    p.copy("act", Cb[d][:, 0:257], C[d][:, 0:257], [BC[d]], [BCb[d]])
            p.barrier()
            ys_, Bys_ = ysm[b], Bysm[b]
            for tt in range(NT if POST_ else 0):
                tok0, ntok = tile_rng(tt)
                p.act(junk[:ntok, :], hsum[:ntok, tt, :], AF.Square, [Bhs], [Bjunk])
                p.op("dve", lambda e, o=ssq[:ntok, 0:1], i_=junk[:ntok, :]: e.reduce_sum(out=o, in_=i_, axis=AX.X), [Bjunk], [Bssq])
                p.act(ssq[:ntok, 0:1], ssq[:ntok, 0:1], AF.Sqrt, [Bssq], [Bssq], bias=self.epsb[:ntok, :], scale=1.0 / 256)
                p.recip(ssq[:ntok, 0:1], ssq[:ntok, 0:1], [Bssq], [Bssq])
                p.stt("dve", tmpy[:ntok, :], hsum[:ntok, tt, :], ssq[:ntok, 0:1], gain[:ntok, hh * 256:(hh + 1) * 256], ALU.mult, ALU.mult, [Bhs, Bssq, self.Bc], [Btmpy])
                p.tt("dve", ym[:ntok, :], tmpy[:ntok, :], oh[b][:ntok, tt, :], ALU.mult, [Btmpy, Bin[b]], [Bym])
                for c2 in range(2):
                    ps1, Bps1 = self.nextps()
                    p.mm([(ps1[:, 0:ntok], ym[:ntok, c2 * 128:(c2 + 1) * 128], self.ident[:ntok, :ntok], True, True)], [Bym, self.Bc], [Bps1])
                    p.copy("act", ys_[:, c2, tok0:tok0 + ntok], ps1[:, 0:ntok], [Bps1], [Bys_], part=True)
            p.dma("sp", t["ymT"][hh * 256:(hh + 1) * 256, :].rearrange("(c p) t -> p c t", p=128), ys_, reads=[Bys_], writes=[Bd["ymT"]], part=True)
            p.barrier()
        p.barrier()

    def phase_merge(self, l, src, dst):
        p, t, ar, ws = self.p, self.t, self.ar, self.ws
        Bd = t["B"]
        for (g0, n) in groups(T, 688):
            ar.reset()
            subs = groups(n)
            Y = [ar.get(NK * n, BF16).rearrange("p (k t) -> p k t", k=NK) for _ in range(3)]
            BY = [Buf("Y%d" % i) for i in range(3)]
            for b, nm in enumerate(("ycT", "yaT", "ymT")):
                p.dma("sp", Y[b], t[nm][:, g0:g0 + n].rearrange("(k p) t -> p k t", p=128), reads=[Bd[nm]], writes=[BY[b]])
            M = ar.get(NK * n, BF16).rearrange("p (k t) -> p k t", k=NK)
            BM = Buf("M")
            G = [ar.get(3 * n, BF16).rearrange("p (b t) -> p b t", b=3) for _ in range(2)]
            BG = [Buf("G0"), Buf("G1")]
            tm = [ar.get(512, F32) for _ in range(2)]
            Btm = [Buf("tm0"), Buf("tm1")]
            tm2 = [ar.get(512, F32) for _ in range(2)]
            Btm2 = [Buf("tm20"), Buf("tm21")]
            hs = [ar.get(n, F32) for _ in range(2)]
            Bhs = [Buf("hs0"), Buf("hs1")]
            names = ("w_conv_proj", "w_attn_proj", "w_mlstm_proj")
            it = 0
            for pj in range(4):
                ww = []
                for b in range(3):
                    ww.append(ws.next((names[b], l, 0, NK, ((pj * 512, 512),)), NK, 512, held=b))
                if p.dry:
                    continue
                for jj in range(4):
                    j = pj * 4 + jj
                    g_, Bg_ = G[j % 2], BG[j % 2]
                    p.dma("sp", g_, t["gT"][:, g0:g0 + n].rearrange("(b k p) t -> p b k t", b=3, p=128)[:, :, j, :], reads=[Bd["gT"]], writes=[Bg_])
                    for (o, s) in subs:
                        pz = []
                        for b in range(3):
                            w, Bw = ww[b]
                            ps, Bps = self.nextps()
                            p.mm([(ps[:, 0:s], w[:, kc, jj * 128:(jj + 1) * 128], Y[b][:, kc, o:o + s], kc == 0, kc == NK - 1) for kc in range(NK)], [Bw, BY[b]], [Bps])
                            pz.append((ps, Bps))
                        a, Ba = tm[it % 2], Btm[it % 2]
                        a2, Ba2 = tm2[it % 2], Btm2[it % 2]
                        it += 1
                        p.tt("dve", a[:, 0:s], pz[0][0][:, 0:s], g_[:, 0, o:o + s], ALU.mult, [pz[0][1], Bg_], [Ba])
                        p.tt("dve", a2[:, 0:s], pz[1][0][:, 0:s], g_[:, 1, o:o + s], ALU.mult, [pz[1][1], Bg_], [Ba2])
                        p.tt("dve", a[:, 0:s], a[:, 0:s], a2[:, 0:s], ALU.add, [Ba, Ba2], [Ba])
                        p.tt("dve", a2[:, 0:s], pz[2][0][:, 0:s], g_[:, 2, o:o + s], ALU.mult, [pz[2][1], Bg_], [Ba2])
                        p.tt("dve", M[:, j, o:o + s], a[:, 0:s], a2[:, 0:s], ALU.add, [Ba, Ba2], [BM], part=True)
            for pj in range(4):
                w, Bw = ws.next(("w_out", l, 0, NK, ((pj * 512, 512),)), NK, 512)
                if p.dry:
                    continue
                for jj in range(4):
                    j = pj * 4 + jj
                    h_, Bh_ = hs[j % 2], Bhs[j % 2]
                    p.dma("sp", h_, src[j * 128:(j + 1) * 128, g0:g0 + n], reads=[Bd["h"]], writes=[Bh_])
                    for (o, s) in subs:
                        ps, Bps = self.nextps()
                        p.mm([(ps[:, 0:s], w[:, kc, jj * 128:(jj + 1) * 128], M[:, kc, o:o + s], kc == 0, kc == NK - 1) for kc in range(NK)], [Bw, BM], [Bps])
                        p.tt("dve", h_[:, o:o + s], h_[:, o:o + s], ps[:, 0:s], ALU.add, [Bps, Bh_], [Bh_], part=True)
                    p.dma("sp", dst[j * 128:(j + 1) * 128, g0:g0 + n], h_, reads=[Bh_], writes=[Bd["h2"]], part=True)
            p.barrier()

    def phase_mlp(self, l, src, dst, final):
        p, t, ar, ws = self.p, self.t, self.ar, self.ws
        Bd = t["B"]
        HB = 8
        for (g0, n) in groups(T, 688):
            ar.reset()
            subs = groups(n)
            hacc = ar.get(NK * n, F32).rearrange("p (k t) -> p k t", k=NK)
            Bh = Buf("hacc")
            xn = ar.get(NK * n, BF16).rearrange("p (k t) -> p k t", k=NK)
            Bxn = Buf("xn2")
            mark = ar.off
            self.rmsnorm(src, g0, n, "gmlp", l, xn, Bxn, keep=hacc, Bkeep=Bh)
            p.barrier()
            ar.off = mark
            actb = ar.get(HB * n, BF16).rearrange("p (k t) -> p k t", k=HB)
            Bact = Buf("act")
            rl = [ar.get(512, F32) for _ in range(2)]
            Brl = [Buf("rl0"), Buf("rl1")]
            it = 0
            for hb in range(8192 // (HB * 128)):
                for half in range(2):
                    w, Bw = ws.next(("mlp_w1", l, 0, NK, ((hb * HB * 128 + half * 512, 512),)), NK, 512)
                    if p.dry:
                        continue
                    for cc in range(4):
                        hc = half * 4 + cc
                        for (o, s) in subs:
                            ps, Bps = self.nextps()
                            p.mm([(ps[:, 0:s], w[:, kc, cc * 128:(cc + 1) * 128], xn[:, kc, o:o + s], kc == 0, kc == NK - 1) for kc in range(NK)], [Bw, Bxn], [Bps])
                            r_, Br_ = rl[it % 2], Brl[it % 2]
                            it += 1
                            p.act(r_[:, 0:s], ps[:, 0:s], AF.Relu, [Bps], [Br_])
                            p.tt("dve", actb[:, hc, o:o + s], r_[:, 0:s], r_[:, 0:s], ALU.mult, [Br_], [Bact], part=True)
                for half in range(2):
                    w, Bw = ws.next(("mlp_w2", l, hb * HB * 128, HB, ((half * 1024, 1024),)), HB, 1024)
                    if p.dry:
                        continue
                    for jj in range(8):
                        j = half * 8 + jj
                        for (o, s) in subs:
                            ps, Bps = self.nextps()
                            p.mm([(ps[:, 0:s], w[:, hc, jj * 128:(jj + 1) * 128], actb[:, hc, o:o + s], hc == 0, hc == HB - 1) for hc in range(HB)], [Bw, Bact], [Bps])
                            p.tt("dve", hacc[:, j, o:o + s], hacc[:, j, o:o + s], ps[:, 0:s], ALU.add, [Bps, Bh], [Bh], part=True)
            if not final:
                p.dma("sp", dst[:, g0:g0 + n].rearrange("(k p) t -> p k t", p=128), hacc, reads=[Bh], writes=[Bd["h"]], part=True)
            else:
                p.barrier()
                ar.off = mark
                sq = [ar.get(n, BF16) for _ in range(2)]
                Bsq = [Buf("fsq0"), Buf("fsq1")]
                rstd = ar.get(n, F32)
                Brs = Buf("frstd")
                pss = [self.nextps() for _ in subs]
                for kc in range(NK):
                    p.act(sq[kc % 2], hacc[:, kc, :], AF.Square, [Bh], [Bsq[kc % 2]])
                    for gi, (o, s) in enumerate(subs):
                        p.mm([(pss[gi][0][:, 0:s], self.ones, sq[kc % 2][:, o:o + s], kc == 0, kc == NK - 1)], [Bsq[kc % 2], self.Bc], [pss[gi][1]], part=kc > 0)
                for gi, (o, s) in enumerate(subs):
                    p.act(rstd[:, o:o + s], pss[gi][0][:, 0:s], AF.Sqrt, [pss[gi][1]], [Brs], bias=self.epsb, scale=1.0 / D, part=True)
                p.recip(rstd, rstd, [Brs], [Brs])
                for kc in range(NK):
                    p.stt("dve", hacc[:, kc, :], hacc[:, kc, :], self.cs("gfin", None, kc), rstd, ALU.mult, ALU.mult, [Bh, Brs, self.Bc], [Bh], part=True)
                p.dma("sp", t["outT"][:, g0:g0 + n].rearrange("(k p) t -> p k t", p=128), hacc, reads=[Bh], writes=[Bd["out"]], part=True)
            p.barrier()

    def build(self, nlayers=DEPTH, stop_after=None):
        p, t = self.p, self.t
        self.sub = None
        if stop_after and ":" in stop_after:
            stop_after, self.sub = stop_after.split(":")
        self.load_consts()
        self.epsb = t["carena"].get(1, F32)
        self.oneb = t["carena"].get(1, F32)
        p.memset("dve", self.epsb, EPS, [self.Bc], part=True)
        p.memset("dve", self.oneb, 1.0, [self.Bc], part=True)
        p.barrier()
        src = t["xT"]
        for l in range(nlayers):
            self.phase_inproj(l, src)
            if stop_after == "inproj":
                break
            self.phase_attn(l)
            if stop_after == "attn":
                break
            self.phase_mlstm(l)
            if stop_after == "mlstm":
                break
            self.phase_merge(l, src, t["hA"])
            if stop_after == "merge":
                break
            self.phase_mlp(l, t["hA"], t["hB"], final=(l == nlayers - 1))
            src = t["hB"]
        p.barrier()


SCRATCH = [("dbgf", [T, 64], F32), ("ycT", [D, T], BF16), ("yaT", [D, T], BF16), ("ymT", [D, T], BF16), ("qkT", [2560, T], BF16),
           ("gT", [3 * D, T], BF16), ("vA", [T, 1024], BF16), ("mq", [T, 1024], BF16), ("mk", [T, 1024], BF16),
           ("mv", [T, 2048], BF16), ("mo", [T, 2048], BF16), ("mg", [T, 32], F32), ("hA", [D, T], F32), ("hB", [D, T], F32)]


def build_program(nu, seq=None, uid=None, dbg=(), nlayers=DEPTH, stop_after=None):
    if seq is None:
        p = Prog(True)
        ws = WStream(p, None, None, None, None)
        t = {"B": {k: Buf(k) for k in ["ycT", "yaT", "ymT", "qkT", "gT", "vA", "mq", "mk", "mv", "mo", "mg", "h", "h2", "out"]}}

        class _A:
            off = 0

            def reset(self):
                pass

            def get(self, n, dt):
                return _FakeAP()
        t["arena"] = _A()
        t["carena"] = _A()
        t["ps"] = [(_FakeAP(), Buf("ps"))] * 8
        for k in ["xT", "outT", "cst", "rep", "tabs", "mats", "matf"] + [s[0] for s in SCRATCH]:
            t[k] = _FakeAP()
        Builder(p, ws, t).build(nlayers, stop_after)
        return ws.rec
    nc = bass.Bass("TRN2", target_bir_lowering=False)
    t = {}
    t["xT"] = nc.dram_tensor("xT", [D, T], F32, kind="ExternalInput").ap()
    t["wst"] = nc.dram_tensor("wst", [nu, 128, WSLOT], F32, kind="ExternalInput").ap()
    t["cst"] = nc.dram_tensor("cst", [128, NCST], F32, kind="ExternalInput").ap()
    t["rep"] = nc.dram_tensor("rep", [128, DEPTH * REP_L], F32, kind="ExternalInput").ap()
    t["tabs"] = nc.dram_tensor("tabs", [128, 2 * T], F32, kind="ExternalInput").ap()
    t["mats"] = nc.dram_tensor("mats", [128, NMAT], F32, kind="ExternalInput").ap()
    t["matf"] = nc.dram_tensor("matf", [128, 384], F32, kind="ExternalInput").ap()
    t["outT"] = nc.dram_tensor("outT", [D, T], F32, kind="ExternalOutput").ap()
    for (nm, shp, dt) in SCRATCH:
        t[nm] = nc.dram_tensor(nm, shp, dt, kind="ExternalOutput" if nm in dbg else "Internal").ap()
    t["B"] = {k: Buf(k) for k in ["ycT", "yaT", "ymT", "qkT", "gT", "vA", "mq", "mk", "mv", "mo", "mg", "h", "h2", "out"]}
    with ExitStack() as st:
        CAR = 40 * 1024
        WB = NSLOT * WSLOT * 2
        total = (nc.sbuf_bytes_remaining - 1024) // 256 * 256
        AR = total - CAR - WB
        big = st.enter_context(nc.sbuf_tensor("big", [128, total // 2], BF16))
        t["carena"] = Arena(big[:, 0:CAR // 2], CAR)
        slots = []
        for i in range(NSLOT):
            o = (CAR + i * WSLOT * 2) // 2
            slots.append((big[:, o:o + WSLOT], Buf("wslot%d" % i)))
        o = (CAR + WB) // 2
        t["arena"] = Arena(big[:, o:o + AR // 2], AR)
        t["ps"] = []
        for i in range(8):
            ps = st.enter_context(nc.psum_tensor("ps%d" % i, [128, 512], F32))
            t["ps"].append((ps[:, :], Buf("ps%d" % i, excl=True)))
        p = Prog(False)
        ws = WStream(p, seq, uid, t["wst"], slots)
        Builder(p, ws, t).build(nlayers, stop_after)
        p.emit(nc)
    return nc


class _FakeAP:
    def __getitem__(self, k):
        return self

    def rearrange(self, *a, **k):
        return self

    def bitcast(self, *a):
        return self


def _panel(W, row0, nkc, segs):
    cols = np.concatenate([np.arange(c0, c0 + n) for (c0, n) in segs])
    sub = W[row0:row0 + nkc * 128][:, cols]
    ncol = sub.shape[1]
    sub = sub.reshape(nkc, 128, ncol).transpose(1, 0, 2).reshape(128, nkc * ncol)
    out = np.zeros((128, WSLOT), np.float32)
    out[:, :nkc * ncol] = sub
    return out


def _const_mats():
    m = np.zeros((128, NMAT), np.float32)
    m[:, M_ID:M_ID + 128] = np.eye(128)
    pm = np.zeros((128, 128), np.float32)
    for mm_ in range(128):
        r = mm_ % 64
        if r < 8:
            pm[mm_ + 8, mm_] = 1.0
        elif r < 16:
            pm[mm_ - 8, mm_] = 1.0
    m[:, M_PERM:M_PERM + 128] = pm
    m[:, M_ONES:M_ONES + 128] = 1.0
    s = np.arange(128)[:, None]
    tq = np.arange(128)[None, :]
    le = (s <= tq).astype(np.float32)
    ge = (s >= tq).astype(np.float32)
    m[:, M_LE4:M_LE4 + 512] = np.tile(le, (1, 4))
    m[:, M_GE4:M_GE4 + 512] = np.tile(ge, (1, 4))
    mq = np.arange(16)[None, :]
    m[:, M_META:M_META + 64] = np.tile((s <= 112 + mq).astype(np.float32), (1, 4))
    mf = np.zeros((128, 384), np.float32)
    mf[:, 0:128] = le
    mf[:, 128:256] = ge
    mf[:, 256:384] = 1.0
    return m, mf


def _rope_tabs():
    pos = np.arange(T, dtype=np.float32)
    inv = (np.float32(500000.0) ** (-np.arange(0, 16, 2, dtype=np.float32) / np.float32(16))).astype(np.float32)
    ang = pos[None, :] * inv[:, None]
    c, s = np.cos(ang).astype(np.float32), np.sin(ang).astype(np.float32)
    ct = np.ones((128, T), np.float32)
    stb = np.zeros((128, T), np.float32)
    for base in (0, 64):
        ct[base:base + 8] = c
        ct[base + 8:base + 16] = c
        stb[base:base + 8] = -s
        stb[base + 8:base + 16] = s
    return np.concatenate([ct, stb], axis=1)


_CACHE = {}


def _program(dbg=(), nlayers=DEPTH, stop_after=None):
    key = (tuple(dbg), nlayers, stop_after)
    if key not in _CACHE:
        rec = build_program(0, None, nlayers=nlayers, stop_after=stop_after)
        uid = {}
        for k, ne in rec:
            if k not in uid:
                uid[k] = len(uid)
        nc = build_program(max(1, len(uid)), rec, uid, dbg, nlayers, stop_after)
        _CACHE[key] = (nc, uid)
    return _CACHE[key]


def _host_inputs(inp, uid):
    f = lambda a: np.ascontiguousarray(np.asarray(a, dtype=np.float32))
    W = {k: f(inp[k]) for k in ("w_in", "w_conv_proj", "w_attn_proj", "w_mlstm_proj", "w_out", "mlp_w1", "mlp_w2")}
    wst = np.zeros((max(1, len(uid)), 128, WSLOT), np.float32)
    for (arr, l, row0, nkc, segs), i in uid.items():
        wst[i] = _panel(W[arr][l], row0, nkc, segs)
    cst = np.zeros((128, NCST), np.float32)
    chunk = lambda v: f(v).reshape(-1, 128).T
    for l in range(DEPTH):
        cst[:, CST_OFF[("gmix", l)]:][:, :16] = chunk(inp["norm_mix"][l])
        cst[:, CST_OFF[("gmlp", l)]:][:, :16] = chunk(inp["norm_mlp"][l])
        cst[:, CST_OFF[("bgate", l)]:][:, :48] = chunk(inp["b_gate"][l])
        cst[:, CST_OFF[("convw", l)]:][:, :48] = chunk(f(inp["conv_w"][l]).reshape(-1))
        cst[:, CST_OFF[("convb", l)]:][:, :16] = chunk(inp["conv_b"][l])
    cst[:, CST_OFF["gfin"]:][:, :16] = chunk(inp["final_norm"])
    rep = np.zeros((128, DEPTH * REP_L), np.float32)
    for l in range(DEPTH):
        bi, bf = f(inp["mlstm_b_i"][l]), f(inp["mlstm_b_f"][l])
        row = np.concatenate([f(inp["mlstm_head_gain"][l]), bi[0], bf[0], bi[1], bf[1], f(inp["attn_sink"][l])])
        rep[:, l * REP_L:(l + 1) * REP_L] = row[None, :]
    mats, matf = _const_mats()
    tabs = _rope_tabs()
    x = f(inp["x"])
    meta = f(inp["meta_tokens"])
    maps = []
    for b in range(NCORES):
        xT = np.ascontiguousarray(np.concatenate([meta, x[b]], axis=0).T)
        maps.append({"xT": xT, "wst": wst, "cst": cst, "rep": rep, "tabs": tabs, "mats": mats, "matf": matf})
    return maps


def kernel(**inputs):
    nc, uid = _program()
    maps = _host_inputs(inputs, uid)
    res = run_bass_kernel_spmd(nc, maps, core_ids=list(range(NCORES)))
    out = np.stack([np.ascontiguousarray(r["outT"].T[LEAD:]) for r in res.results], axis=0)
    return out.astype(np.float32)
```
